# Optimizing a Trainium2 kernel written in Bass

```python
import math
import jax
import jax.numpy as jnp
from jax import lax
import numpy as np

D_MODEL = 1024
BATCH = 2
SEQ = 16384
DEPTH = 4

HEAD_DIM = 64
N_GROUPS_MIX = 4
GROUP_WIDTH = D_MODEL // N_GROUPS_MIX
N_SB_HEADS = GROUP_WIDTH // HEAD_DIM
N_SWA_HEADS = GROUP_WIDTH // HEAD_DIM
N_SWA_KV_HEADS = 2
N_FOX_HEADS = GROUP_WIDTH // HEAD_DIM
SSM_CH_PER_GROUP = 16
N_SSM_GROUPS = GROUP_WIDTH // SSM_CH_PER_GROUP
SSM_STATE = 64
WINDOW = 128
Q_BLOCK = 128
REL_BUCKETS = 32
REL_MAX_DIST = 128
D_FF = 4 * D_MODEL
NORM_EPS = 1e-6
DT_MIN = 1e-3
DT_MAX = 1e-1
N_ADA = 6

IN_SIZES = (GROUP_WIDTH, GROUP_WIDTH, GROUP_WIDTH,
            N_SWA_HEADS * HEAD_DIM, N_SWA_KV_HEADS * HEAD_DIM, N_SWA_KV_HEADS * HEAD_DIM,
            GROUP_WIDTH, GROUP_WIDTH, GROUP_WIDTH, N_FOX_HEADS,
            GROUP_WIDTH)
IN_WIDTH = sum(IN_SIZES)

STRICT_LOWER = np.tril(np.ones((Q_BLOCK, Q_BLOCK), dtype=bool), -1)
LOWER_INCL = np.tril(np.ones((Q_BLOCK, Q_BLOCK), dtype=bool), 0)
SUFFIX = np.tril(np.ones((Q_BLOCK, Q_BLOCK), dtype=np.float32), -1)

kernel_name = "hybrid_parallel_sb_swa_fox_s5"

F32 = jnp.float32


def rmsnorm(x, gain):
    x32 = x.astype(F32)
    y = x32 * lax.rsqrt(jnp.mean(x32 * x32, axis=-1, keepdims=True) + NORM_EPS)
    return (y * gain.astype(F32)).astype(x.dtype)


def to_query_blocks(t):
    b, s, h, d = t.shape
    return t.reshape(b, s // Q_BLOCK, Q_BLOCK, h, d).transpose(1, 0, 3, 2, 4)


def from_query_blocks(o):
    n, b, h, qb, d = o.shape
    return o.transpose(1, 0, 3, 2, 4).reshape(b, n * qb, h * d)


def causal_block_pairs(n):
    counts = np.arange(1, n + 1)
    qb = np.repeat(np.arange(n), counts)
    starts = np.repeat(np.cumsum(counts) - counts, counts)
    kb = qb - (np.arange(qb.size) - starts)
    return jnp.asarray(qb, jnp.int32), jnp.asarray(kb, jnp.int32)


def _take(blocks, i):
    return lax.dynamic_index_in_dim(blocks, i, 0, keepdims=False)


def stick_breaking_attention(q, k, v):
    b, s, h, d = q.shape
    n = s // Q_BLOCK
    scale = 1.0 / math.sqrt(d)
    qblk = to_query_blocks(q.astype(F32))
    kblk = to_query_blocks(k.astype(F32))
    vblk = to_query_blocks(v.astype(F32))
    strict = jnp.asarray(STRICT_LOWER)
    suffix = jnp.asarray(SUFFIX)

    def step(carry, idx):
        out, acc, run = carry
        qi, ki = idx
        first = qi == ki
        z = jnp.einsum('bhqd,bhkd->bhqk', _take(qblk, qi), _take(kblk, ki)) * scale
        mask = jnp.where(first, strict, True)
        lsz = jax.nn.log_sigmoid(z)
        lk = jnp.where(mask, lsz - z, 0.0)
        within = jnp.einsum('bhqk,kj->bhqj', lk, suffix)
        acc0 = jnp.where(first, 0.0, acc)
        run0 = jnp.where(first, 0.0, run)
        w = jnp.where(mask, jnp.exp(lsz + within + run0[..., None]), 0.0)
        acc = acc0 + jnp.einsum('bhqk,bhkd->bhqd', w, _take(vblk, ki))
        run = run0 + jnp.sum(lk, axis=-1)
        out = lax.dynamic_update_index_in_dim(out, acc, qi, 0)
        return (out, acc, run), None

    init = (jnp.zeros((n, b, h, Q_BLOCK, d), F32),
            jnp.zeros((b, h, Q_BLOCK, d), F32),
            jnp.zeros((b, h, Q_BLOCK), F32))
    (out, _, _), _ = lax.scan(step, init, causal_block_pairs(n))
    return from_query_blocks(out).astype(q.dtype)


def forgetting_attention(q, k, v, log_f):
    b, s, h, d = q.shape
    n = s // Q_BLOCK
    scale = 1.0 / math.sqrt(d)
    cum = jnp.cumsum(log_f.astype(F32), axis=1)
    fblk = cum.reshape(b, n, Q_BLOCK, h).transpose(1, 0, 3, 2)
    qblk = to_query_blocks(q.astype(F32))
    kblk = to_query_blocks(k.astype(F32))
    vblk = to_query_blocks(v.astype(F32))
    lower = jnp.asarray(LOWER_INCL)

    def step(carry, idx):
        out, acc, m, l = carry
        qi, ki = idx
        first = qi == ki
        z = jnp.einsum('bhqd,bhkd->bhqk', _take(qblk, qi), _take(kblk, ki)) * scale
        z = z + _take(fblk, qi)[..., None] - _take(fblk, ki)[..., None, :]
        mask = jnp.where(first, lower, True)
        zm = jnp.where(mask, z, -jnp.inf)
        m0 = jnp.where(first, -jnp.inf, m)
        l0 = jnp.where(first, 0.0, l)
        acc0 = jnp.where(first, 0.0, acc)
        m_new = jnp.maximum(m0, jnp.max(zm, axis=-1))
        corr = jnp.exp(m0 - m_new)
        e = jnp.exp(zm - m_new[..., None])
        l = l0 * corr + jnp.sum(e, axis=-1)
        acc = acc0 * corr[..., None] + jnp.einsum('bhqk,bhkd->bhqd', e, _take(vblk, ki))
        out = lax.dynamic_update_index_in_dim(out, acc / l[..., None], qi, 0)
        return (out, acc, m_new, l), None

    init = (jnp.zeros((n, b, h, Q_BLOCK, d), F32),
            jnp.zeros((b, h, Q_BLOCK, d), F32),
            jnp.full((b, h, Q_BLOCK), -jnp.inf, F32),
            jnp.zeros((b, h, Q_BLOCK), F32))
    (out, _, _, _), _ = lax.scan(step, init, causal_block_pairs(n))
    return from_query_blocks(out).astype(q.dtype)


def t5_causal_buckets(dist):
    max_exact = REL_BUCKETS // 2
    safe = np.maximum(dist, 1).astype(np.float32)
    large = max_exact + (np.log(safe / max_exact) / math.log(REL_MAX_DIST / max_exact)
                         * (REL_BUCKETS - max_exact)).astype(np.int32)
    large = np.minimum(large, REL_BUCKETS - 1)
    return np.where(dist < max_exact, dist, large).astype(np.int32)


def sliding_window_attention(q, k, v, rel_table, sink):
    b, s, hq, d = q.shape
    hkv = k.shape[2]
    g = hq // hkv
    n = s // WINDOW
    qb = q.astype(F32).reshape(b, n, WINDOW, hkv, g, d)

    def band(t):
        t = t.astype(F32).reshape(b, n, WINDOW, hkv, d)
        prev = jnp.pad(t, ((0, 0), (1, 0), (0, 0), (0, 0), (0, 0)))[:, :-1]
        return jnp.concatenate([prev, t], axis=2)

    kb, vb = band(k), band(v)
    z = jnp.einsum('bnqhgd,bnkhd->bnhgqk', qb, kb) / math.sqrt(d)
    i = np.arange(WINDOW)[:, None]
    j = np.arange(2 * WINDOW)[None, :]
    dist = WINDOW + i - j
    in_window = (dist >= 0) & (dist < WINDOW)
    bucket = t5_causal_buckets(np.clip(dist, 0, None))
    bias = rel_table.astype(F32)[bucket]
    bias = bias.transpose(2, 0, 1).reshape(hkv, g, WINDOW, 2 * WINDOW)
    key_exists = (jnp.arange(n)[:, None] > 0) | (j[0] >= WINDOW)[None, :]
    valid = in_window[None] & key_exists[:, None, :]
    z = jnp.where(valid[None, :, None, None], z + bias, -jnp.inf)
    sink_col = jnp.broadcast_to(sink.astype(F32).reshape(1, 1, hkv, g, 1, 1),
                                z.shape[:-1] + (1,))
    probs = jax.nn.softmax(jnp.concatenate([z, sink_col], axis=-1), axis=-1)[..., :-1]
    o = jnp.einsum('bnhgqk,bnkhd->bnqhgd', probs, vb)
    return o.reshape(b, s, hq * d).astype(q.dtype)


def _ssm_combine(e1, e2):
    a1r, a1i, b1r, b1i = e1
    a2r, a2i, b2r, b2i = e2
    ar = a2r * a1r - a2i * a1i
    ai = a2r * a1i + a2i * a1r
    br = a2r * b1r - a2i * b1i + b2r
    bi = a2r * b1i + a2i * b1r + b2i
    return (ar, ai, br, bi)


def s5_ssm(u, lam_re, lam_im, log_dt, b_re, b_im, c_re, c_im, d_skip, w_glu, b_glu):
    bsz, s, _ = u.shape
    u32 = u.astype(F32).reshape(bsz, s, N_SSM_GROUPS, SSM_CH_PER_GROUP)
    dt = jnp.exp(log_dt.astype(F32))[:, None]
    lr, li = lam_re.astype(F32), lam_im.astype(F32)
    mag = jnp.exp(lr * dt)
    ang = li * dt
    a_re, a_im = mag * jnp.cos(ang), mag * jnp.sin(ang)
    den = lr * lr + li * li
    nr, ni = a_re - 1.0, a_im
    coef_re = (nr * lr + ni * li) / den
    coef_im = (ni * lr - nr * li) / den
    br, bi = b_re.astype(F32), b_im.astype(F32)
    bb_re = coef_re[..., None] * br - coef_im[..., None] * bi
    bb_im = coef_re[..., None] * bi + coef_im[..., None] * br
    bu_re = jnp.einsum('bsgh,gph->bsgp', u32, bb_re)
    bu_im = jnp.einsum('bsgh,gph->bsgp', u32, bb_im)
    a_re_t = jnp.broadcast_to(a_re, bu_re.shape)
    a_im_t = jnp.broadcast_to(a_im, bu_re.shape)
    _, _, x_re, x_im = lax.associative_scan(_ssm_combine, (a_re_t, a_im_t, bu_re, bu_im), axis=1)
    y = (jnp.einsum('bsgp,ghp->bsgh', x_re, c_re.astype(F32))
         - jnp.einsum('bsgp,ghp->bsgh', x_im, c_im.astype(F32))
         + d_skip.astype(F32) * u32)
    y = jax.nn.gelu(y.reshape(bsz, s, N_SSM_GROUPS * SSM_CH_PER_GROUP))
    gate = jax.nn.sigmoid(y @ w_glu.astype(F32) + b_glu.astype(F32))
    return (y * gate).astype(u.dtype)


def setup_inputs(seed: int = 0) -> dict:
    key = jax.random.key(seed)
    ks = jax.random.split(key, 32)
    nrm = lambda k, shape, s=1.0: jax.random.normal(k, shape, F32) * s
    P, G, H = SSM_STATE, N_SSM_GROUPS, SSM_CH_PER_GROUP
    lam_im_base = math.pi * jnp.arange(P, dtype=F32)
    return {
        "x": nrm(ks[0], (BATCH, SEQ, D_MODEL)),
        "c": nrm(ks[1], (BATCH, D_MODEL)),
        "w_ada": nrm(ks[2], (DEPTH, D_MODEL, N_ADA * D_MODEL), 0.5 * D_MODEL ** -0.5),
        "b_ada": nrm(ks[3], (DEPTH, N_ADA * D_MODEL), 0.01),
        "norm1_gain": 1.0 + nrm(ks[4], (DEPTH, D_MODEL), 0.02),
        "norm2_gain": 1.0 + nrm(ks[5], (DEPTH, D_MODEL), 0.02),
        "w_in": nrm(ks[6], (DEPTH, D_MODEL, IN_WIDTH), D_MODEL ** -0.5),
        "rel_bias": nrm(ks[7], (REL_BUCKETS, N_SWA_HEADS), 0.5),
        "sinks": nrm(ks[8], (DEPTH, N_SWA_HEADS)),
        "forget_bias": 3.0 + nrm(ks[9], (DEPTH, N_FOX_HEADS), 0.5),
        "lam_re": -0.5 + nrm(ks[10], (DEPTH, G, P), 0.01),
        "lam_im": lam_im_base + nrm(ks[11], (DEPTH, G, P), 0.01),
        "log_dt": jax.random.uniform(ks[12], (DEPTH, G), F32, math.log(DT_MIN), math.log(DT_MAX)),
        "ssm_b_re": nrm(ks[13], (DEPTH, G, P, H), (2 * H) ** -0.5),
        "ssm_b_im": nrm(ks[14], (DEPTH, G, P, H), (2 * H) ** -0.5),
        "ssm_c_re": nrm(ks[15], (DEPTH, G, H, P), P ** -0.5),
        "ssm_c_im": nrm(ks[16], (DEPTH, G, H, P), P ** -0.5),
        "ssm_d": nrm(ks[17], (DEPTH, G, H)),
        "w_glu": nrm(ks[18], (DEPTH, GROUP_WIDTH, GROUP_WIDTH), GROUP_WIDTH ** -0.5),
        "b_glu": nrm(ks[19], (DEPTH, GROUP_WIDTH), 0.01),
        "out_gain": 1.0 + nrm(ks[20], (DEPTH, D_MODEL), 0.02),
        "w_out": nrm(ks[21], (DEPTH, D_MODEL, D_MODEL), D_MODEL ** -0.5),
        "w_mlp_in": nrm(ks[22], (DEPTH, D_MODEL, D_FF), D_MODEL ** -0.5),
        "w_mlp_out": nrm(ks[23], (DEPTH, D_FF, D_MODEL), D_FF ** -0.5),
        "final_gain": 1.0 + nrm(ks[24], (D_MODEL,), 0.02),
    }


def reference(x, c, w_ada, b_ada, norm1_gain, norm2_gain, w_in, rel_bias, sinks,
              forget_bias, lam_re, lam_im, log_dt, ssm_b_re, ssm_b_im, ssm_c_re,
              ssm_c_im, ssm_d, w_glu, b_glu, out_gain, w_out, w_mlp_in, w_mlp_out,
              final_gain):
    bsz, s, _ = x.shape
    split_points = [int(p) for p in np.cumsum(IN_SIZES)[:-1]]
    heads = lambda t, h: t.reshape(bsz, s, h, HEAD_DIM)
    c_act = jax.nn.silu(c)
    for l in range(DEPTH):
        mod = c_act @ w_ada[l] + b_ada[l]
        sh1, sc1, g1, sh2, sc2, g2 = [m[:, None, :] for m in jnp.split(mod, N_ADA, axis=-1)]

        h = rmsnorm(x, norm1_gain[l]) * (1.0 + sc1) + sh1
        proj = h @ w_in[l]
        (sb_q, sb_k, sb_v, sw_q, sw_k, sw_v,
         fx_q, fx_k, fx_v, fx_f, ssm_u) = jnp.split(proj, split_points, axis=-1)

        o_sb = stick_breaking_attention(heads(sb_q, N_SB_HEADS), heads(sb_k, N_SB_HEADS),
                                        heads(sb_v, N_SB_HEADS))
        o_sw = sliding_window_attention(heads(sw_q, N_SWA_HEADS), heads(sw_k, N_SWA_KV_HEADS),
                                        heads(sw_v, N_SWA_KV_HEADS), rel_bias, sinks[l])
        log_f = jax.nn.log_sigmoid(fx_f.astype(F32) + forget_bias[l].astype(F32))
        o_fx = forgetting_attention(heads(fx_q, N_FOX_HEADS), heads(fx_k, N_FOX_HEADS),
                                    heads(fx_v, N_FOX_HEADS), log_f)
        o_ssm = s5_ssm(ssm_u, lam_re[l], lam_im[l], log_dt[l], ssm_b_re[l], ssm_b_im[l],
                       ssm_c_re[l], ssm_c_im[l], ssm_d[l], w_glu[l], b_glu[l])

        mixed = jnp.concatenate([o_sb, o_sw, o_fx, o_ssm], axis=-1)
        mixed = rmsnorm(mixed.reshape(bsz, s, N_GROUPS_MIX, GROUP_WIDTH),
                        out_gain[l].reshape(N_GROUPS_MIX, GROUP_WIDTH)).reshape(bsz, s, D_MODEL)
        x = x + g1 * (mixed @ w_out[l])

        h = rmsnorm(x, norm2_gain[l]) * (1.0 + sc2) + sh2
        x = x + g2 * (jnp.square(jax.nn.relu(h @ w_mlp_in[l])) @ w_mlp_out[l])
    return rmsnorm(x, final_gain)
```

```python
import numpy as np
import concourse.bass as bass
import concourse.mybir as mybir
from concourse.bass_utils import run_bass_kernel_spmd
from contextlib import ExitStack

F32 = mybir.dt.float32
BF16 = mybir.dt.bfloat16
AF = mybir.ActivationFunctionType
ALU = mybir.AluOpType


class Prog:
    ENGS = ('pe', 'act', 'dve', 'pool', 'sp')

    def __init__(self, nc, es, n_dma_sems=24):
        self.nc = nc
        self.es = es
        self.sem = {e: es.enter_context(nc.semaphore("sem_" + e)) for e in self.ENGS}
        self.cnt = {e: 0 for e in self.ENGS}
        self.streams = {e: [] for e in self.ENGS}
        self.seen = {e: {} for e in self.ENGS}
        self.dma_sems = [es.enter_context(nc.semaphore(f"dsem{i}")) for i in range(n_dma_sems)]
        self.dma_val = [0] * n_dma_sems
        self.dma_next = 0
        self.last_w = {}
        self.readers = {}
        self.nbuf = 0

    def sb(self, name, shape, dt):
        return self.es.enter_context(self.nc.sbuf_tensor(name, list(shape), dt))

    def ps(self, name, shape, dt=F32):
        return self.es.enter_context(self.nc.psum_tensor(name, list(shape), dt))

    def _need(self, eng, tok, waits):
        kind, a, v = tok
        key = (kind, a)
        if self.seen[eng].get(key, 0) >= v:
            return
        self.seen[eng][key] = v
        waits.append(tok)

    def _deps(self, eng, reads, writes, waits):
        for b in reads:
            t = self.last_w.get(b)
            if t is not None:
                self._need(eng, t, waits)
        for b in writes:
            t = self.last_w.get(b)
            if t is not None and not (t[0] == 'e' and t[1] == eng):
                self._need(eng, t, waits)
            for t in self.readers.get(b, {}).values():
                if not (t[0] == 'e' and t[1] == eng):
                    self._need(eng, t, waits)

    def _record(self, tok, reads, writes):
        for b in reads:
            self.readers.setdefault(b, {})[(tok[0], tok[1])] = tok
        for b in writes:
            self.last_w[b] = tok
            self.readers[b] = {}

    def op(self, eng, fn, reads=(), writes=()):
        waits = []
        self._deps(eng, reads, writes, waits)
        self.cnt[eng] += 1
        tok = ('e', eng, self.cnt[eng])
        self._record(tok, reads, writes)
        self.streams[eng].append((waits, fn, None))

    def dma(self, queue, out_ap, in_ap, reads=(), writes=()):
        i = self.dma_next
        self.dma_next = (i + 1) % len(self.dma_sems)
        waits = []
        if self.dma_val[i] > 0:
            self._need(queue, ('d', i, self.dma_val[i]), waits)
        self._deps(queue, reads, writes, waits)
        self.dma_val[i] += 16
        tok = ('d', i, self.dma_val[i])
        self._record(tok, reads, writes)
        self.streams[queue].append((waits, lambda E: E.dma_start(out=out_ap, in_=in_ap), i))

    def finish(self, eng='sp'):
        waits = []
        for i, v in enumerate(self.dma_val):
            if v > 0:
                self._need(eng, ('d', i, v), waits)
        for e in self.ENGS:
            if e != eng and self.cnt[e] > 0:
                self._need(eng, ('e', e, self.cnt[e]), waits)
        self.streams[eng].append((waits, None, None))

    def emit(self):
        nc = self.nc
        with nc.Block() as block:
            decos = {'pe': block.tensor, 'act': block.scalar, 'dve': block.vector,
                     'pool': block.gpsimd, 'sp': block.sync}
            for e in self.ENGS:
                stream = self.streams[e]

                def body(E, stream=stream, e=e):
                    for waits, fn, d in stream:
                        for (kind, a, v) in waits:
                            E.wait_ge(self.sem[a] if kind == 'e' else self.dma_sems[a], v)
                        if fn is None:
                            continue
                        ins = fn(E)
                        if d is None:
                            ins.then_inc(self.sem[e], 1)
                        else:
                            ins.then_inc(self.dma_sems[d], 16)
                decos[e](body)


NORM_EPS = 1e-6
D = 1024
FM_CHUNKS = [(0, 128, 0.125), (128, 128, 0.125),
             (256, 128, 1.0), (384, 128, 1.0),
             (768, 128, 0.125), (896, 128, 0.125),
             (1024, 128, 1.0),
             (1280, 128, 0.125), (1408, 128, 0.125),
             (1536, 128, 1.0), (1664, 128, 1.0),
             (2052, 128, 1.0), (2180, 128, 1.0)]
V_COLS = [(512, 256), (1152, 128), (1792, 256)]


def rms_stats(P, X, sq, ones_bf, ps_ss, rstd, nchunks, TT, tag, inv_n):
    P.op('act', lambda E: E.activation(out=sq[:, 0:nchunks, :], in_=X, func=AF.Square),
         reads=[tag + 'X'], writes=[tag + 'sq'])
    for c in range(nchunks):
        P.op('pe', lambda E, c=c: E.matmul(ps_ss[:, :], ones_bf[:, :], sq[:, c, :],
                                           start=(c == 0), stop=(c == nchunks - 1)),
             reads=[tag + 'sq'], writes=[tag + 'ss'])
    P.op('act', lambda E: E.activation(out=rstd[:, :], in_=ps_ss[:, :], func=AF.Sqrt,
                                       bias=NORM_EPS, scale=inv_n),
         reads=[tag + 'ss'], writes=[tag + 'rstd'])
    P.op('dve', lambda E: E.reciprocal(out=rstd[:, :], in_=rstd[:, :]),
         reads=[tag + 'rstd'], writes=[tag + 'rstd'])


def build_stageA(T, TT=512):
    nc = bass.Bass("TRN2", target_bir_lowering=False)
    xT = nc.dram_tensor("xT", [D, T], F32, kind="ExternalInput").ap()
    win = nc.dram_tensor("win", [D, 2308], F32, kind="ExternalInput").ap()
    modv = nc.dram_tensor("modv", [128, 3, 8], F32, kind="ExternalInput").ap()
    qkT = nc.dram_tensor("qkT", [13, 128, T], BF16, kind="ExternalOutput").ap()
    fT = nc.dram_tensor("fT", [4, T], F32, kind="ExternalOutput").ap()
    vtok = nc.dram_tensor("vtok", [T, 640], BF16, kind="ExternalOutput").ap()
    NT = T // TT
    with ExitStack() as es:
        P = Prog(nc, es)
        W = P.sb("W", [128, 8, 2308], BF16)
        WV = P.sb("WV", [128, 8, 640], BF16)
        mv = P.sb("mv", [128, 3, 8], F32)
        gsc = P.sb("gsc", [128, 8], F32)
        ones_bf = P.sb("ones_bf", [128, 128], BF16)
        Xs = [P.sb(f"X{i}", [128, 8, TT], F32) for i in range(2)]
        sq = P.sb("sq", [128, 8, TT], BF16)
        rstd = P.sb("rstd", [128, TT], F32)
        tmp = [P.sb(f"tmp{i}", [128, TT], F32) for i in range(2)]
        hT = P.sb("hT", [128, 8, TT], BF16)
        ost = [P.sb(f"ost{i}", [128, TT], BF16) for i in range(3)]
        fst = P.sb("fst", [4, TT], F32)
        vst = [P.sb(f"vst{i}", [128, 640], BF16) for i in range(2)]
        ps_ss = P.ps("ps_ss", [128, TT])
        ps_c = [P.ps(f"ps_c{i}", [128, TT]) for i in range(3)]
        ps_v = [P.ps(f"ps_v{i}", [128, 1024]) for i in range(2)]

        xv = xT.rearrange("(c p) t -> p c t", p=128)
        wv_ = win.rearrange("(c p) n -> p c n", p=128)
        P.dma('sp', mv[:, :, :], modv[:, :, :], writes=['mv'])
        for c in range(8):
            P.dma('pool', W[:, c, :], wv_[:, c, :], writes=['W'])
            off = 0
            for (c0, n) in V_COLS:
                P.dma('pool', WV[:, c, off:off + n], wv_[:, c, c0:c0 + n], writes=['WV'])
                off += n
        P.op('pool', lambda E: E.memset(ones_bf[:, :], 1.0), writes=['ones'])
        P.op('dve', lambda E: E.scalar_tensor_tensor(out=gsc[:, :], in0=mv[:, 1, :], scalar=1.0,
                                                     in1=mv[:, 0, :], op0=ALU.add, op1=ALU.mult),
             reads=['mv'], writes=['gsc'])
        nev = 0
        for it in range(NT):
            X = Xs[it % 2]
            xk = f"X{it % 2}"
            t0 = it * TT
            P.dma('sp', X[:, :, :], xv[:, :, t0:t0 + TT], writes=[xk])
            P.op('act', lambda E, X=X: E.activation(out=sq[:, :, :], in_=X[:, :, :], func=AF.Square),
                 reads=[xk], writes=['sq'])
            for c in range(8):
                P.op('pe', lambda E, c=c: E.matmul(ps_ss[:, :], ones_bf[:, :], sq[:, c, :],
                                                   start=(c == 0), stop=(c == 7)),
                     reads=['sq', 'ones'], writes=['ss'])
            P.op('act', lambda E: E.activation(out=rstd[:, :], in_=ps_ss[:, :], func=AF.Sqrt,
                                               bias=NORM_EPS, scale=1.0 / D),
                 reads=['ss'], writes=['rstd'])
            P.op('dve', lambda E: E.reciprocal(out=rstd[:, :], in_=rstd[:, :]),
                 reads=['rstd'], writes=['rstd'])
            for c in range(8):
                tb = tmp[c % 2]
                tk = f"tmp{c % 2}"
                P.op('dve', lambda E, X=X, c=c, tb=tb: E.scalar_tensor_tensor(
                    out=tb[:, :], in0=X[:, c, :], scalar=gsc[:, c:c + 1], in1=rstd[:, :],
                    op0=ALU.mult, op1=ALU.mult), reads=[xk, 'gsc', 'rstd'], writes=[tk])
                P.op('act', lambda E, c=c, tb=tb: E.activation(
                    out=hT[:, c, :], in_=tb[:, :], func=AF.Identity, bias=mv[:, 2, c:c + 1], scale=1.0),
                    reads=[tk, 'mv'], writes=[f'hT'])
            for j, (c0, n, scl) in enumerate(FM_CHUNKS):
                pb = ps_c[nev % 3]; pk = f"psc{nev % 3}"
                ob = ost[nev % 3]; ok = f"ost{nev % 3}"
                for c in range(8):
                    P.op('pe', lambda E, c=c, c0=c0, n=n, pb=pb: E.matmul(
                        pb[:, :], W[:, c, c0:c0 + n], hT[:, c, :], start=(c == 0), stop=(c == 7)),
                        reads=['W', 'hT'], writes=[pk])
                if nev % 2 == 0:
                    P.op('act', lambda E, pb=pb, ob=ob, scl=scl: E.activation(
                        out=ob[:, :], in_=pb[:, :], func=AF.Copy, scale=scl), reads=[pk], writes=[ok])
                else:
                    P.op('dve', lambda E, pb=pb, ob=ob, scl=scl: E.tensor_scalar(
                        out=ob[:, :], in0=pb[:, :], scalar1=scl, scalar2=None, op0=ALU.mult),
                        reads=[pk], writes=[ok])
                P.dma('sp', qkT[j, :, t0:t0 + TT], ob[:, :], reads=[ok])
                nev += 1
            pb = ps_c[nev % 3]; pk = f"psc{nev % 3}"
            for c in range(8):
                P.op('pe', lambda E, c=c, pb=pb: E.matmul(
                    pb[0:4, :], W[:, c, 2048:2052], hT[:, c, :], start=(c == 0), stop=(c == 7)),
                    reads=['W', 'hT'], writes=[pk])
            P.op('act', lambda E, pb=pb: E.activation(out=fst[:, :], in_=pb[0:4, :], func=AF.Copy),
                 reads=[pk], writes=['fst'])
            P.dma('sp', fT[:, t0:t0 + TT], fst[:, :], reads=['fst'])
            nev += 1
            for s in range(TT // 128):
                pv = ps_v[s % 2]; pvk = f"psv{s % 2}"
                vb = vst[s % 2]; vk = f"vst{s % 2}"
                for c in range(8):
                    P.op('pe', lambda E, c=c, s=s, pv=pv: E.matmul(
                        pv[:, 0:512], hT[:, c, s * 128:(s + 1) * 128], WV[:, c, 0:512],
                        start=(c == 0), stop=(c == 7)), reads=['WV', 'hT'], writes=[pvk])
                for c in range(8):
                    P.op('pe', lambda E, c=c, s=s, pv=pv: E.matmul(
                        pv[:, 512:640], hT[:, c, s * 128:(s + 1) * 128], WV[:, c, 512:640],
                        start=(c == 0), stop=(c == 7)), reads=['WV', 'hT'], writes=[pvk])
                P.op('act', lambda E, pv=pv, vb=vb: E.activation(out=vb[:, 0:320], in_=pv[:, 0:320], func=AF.Copy),
                     reads=[pvk], writes=[vk])
                P.op('dve', lambda E, pv=pv, vb=vb: E.tensor_copy(out=vb[:, 320:640], in_=pv[:, 320:640]),
                     reads=[pvk], writes=[vk])
                P.dma('sp', vtok[t0 + s * 128:t0 + (s + 1) * 128, :], vb[:, :], reads=[vk])
        P.finish('sp')
        P.emit()
    return nc


def build_sb(S):
    nc = bass.Bass("TRN2", target_bir_lowering=False)
    qT_d = nc.dram_tensor("qT", [64, S], BF16, kind="ExternalInput").ap()
    kT_d = nc.dram_tensor("kT", [64, S], BF16, kind="ExternalInput").ap()
    v_d = nc.dram_tensor("v", [S, 64], BF16, kind="ExternalInput").ap()
    cst_d = nc.dram_tensor("cst", [128, 3, 128], BF16, kind="ExternalInput").ap()
    o_d = nc.dram_tensor("o", [S, 64], F32, kind="ExternalOutput").ap()
    NB = S // 128
    NG = NB // 4
    with ExitStack() as es:
        P = Prog(nc, es)
        QT = P.sb("QT", [64, S], BF16)
        KT = P.sb("KT", [64, S], BF16)
        V = P.sb("V_sb", [128, NB, 64], BF16)
        cst = P.sb("cst_sb", [128, 3, 128], BF16)
        zer = P.sb("zer", [128, 256], BF16)
        Rsb = P.sb("Rsb", [128, 512], F32)
        e_ = [P.sb(f"e{i}", [128, 512], F32) for i in range(2)]
        sp_ = [P.sb(f"sp{i}", [128, 512], BF16) for i in range(2)]
        arg_ = [P.sb(f"arg{i}", [128, 512], F32) for i in range(2)]
        w_ = [P.sb(f"w{i}", [128, 512], BF16) for i in range(2)]
        ob_ = [P.sb(f"ob{i}", [128, 4, 64], F32) for i in range(2)]
        P1_ = [P.ps(f"P1_{i}", [128, 512]) for i in range(2)]
        P2_ = [P.ps(f"P2_{i}", [128, 512]) for i in range(2)]
        acc_ = [P.ps(f"acc{i}", [128, 512]) for i in range(2)]
        nchunk = max(1, S // 4096)
        cs = S // nchunk
        for i in range(nchunk):
            P.dma('sp', QT[:, i * cs:(i + 1) * cs], qT_d[:, i * cs:(i + 1) * cs], writes=['QT'])
            P.dma('sp', KT[:, i * cs:(i + 1) * cs], kT_d[:, i * cs:(i + 1) * cs], writes=['KT'])
        P.dma('sp', V[:, :, :], v_d.rearrange("(n p) d -> p n d", p=128), writes=['V'])
        P.dma('sp', cst[:, :, :], cst_d[:, :, :], writes=['cst'])
        P.op('pool', lambda E: E.memset(zer[:, :], 0.0), writes=['zer'])
        NINCL = cst[:, 0, :]
        NONES = cst[:, 1, :]
        MASK = cst[:, 2, :]
        u = 0
        for G in range(NG):
            acc = acc_[G % 2]; ak = f"acc{G % 2}"
            ob = ob_[G % 2]; obk = f"ob{G % 2}"
            P.op('pool', lambda E: E.memset(Rsb[:, :], 0.0), writes=['Rsb'])
            P.op('pe', lambda E, acc=acc: E.matmul(acc[:, 0:256], zer[:, 0:128], zer[:, 0:256], start=True, stop=False),
                 reads=['zer'], writes=[ak])
            for kb in range(4 * G + 3, -1, -1):
                j0 = max(0, kb - 4 * G)
                diag = kb >= 4 * G
                c0 = j0 * 128
                s = u % 2
                u += 1
                P1 = P1_[s]; P2 = P2_[s]; e = e_[s]; sp = sp_[s]; arg = arg_[s]; w = w_[s]
                k1, k2, ke, ks, ka, kw = f"P1{s}", f"P2{s}", f"e{s}", f"sp{s}", f"arg{s}", f"w{s}"
                q0 = 512 * G
                P.op('pe', lambda E, P1=P1, kb=kb, c0=c0, q0=q0: E.matmul(
                    P1[:, c0:512], KT[:, kb * 128:(kb + 1) * 128], QT[:, q0 + c0:q0 + 512], start=True, stop=False),
                    reads=['KT', 'QT'], writes=[k1])
                P.op('act', lambda E, P1=P1, e=e, c0=c0: E.activation(out=e[:, c0:512], in_=P1[:, c0:512], func=AF.Exp),
                     reads=[k1], writes=[ke])
                P.op('act', lambda E, sp=sp, e=e, c0=c0: E.activation(out=sp[:, c0:512], in_=e[:, c0:512], func=AF.Ln, bias=1.0, scale=1.0),
                     reads=[ke], writes=[ks])
                if diag:
                    P.op('pool', lambda E, sp=sp, c0=c0: E.tensor_tensor(out=sp[:, c0:c0 + 128], in0=sp[:, c0:c0 + 128], in1=MASK, op=ALU.mult),
                         reads=[ks, 'cst'], writes=[ks])
                P.op('pe', lambda E, P1=P1, sp=sp, c0=c0: E.matmul(P1[:, c0:512], NINCL, sp[:, c0:512], start=False, stop=True),
                     reads=[ks, 'cst'], writes=[k1])
                P.op('pe', lambda E, P2=P2, sp=sp, c0=c0: E.matmul(P2[:, c0:512], NONES, sp[:, c0:512], start=True, stop=True),
                     reads=[ks, 'cst'], writes=[k2])
                P.op('dve', lambda E, P1=P1, arg=arg, c0=c0: E.tensor_tensor(out=arg[:, c0:512], in0=P1[:, c0:512], in1=Rsb[:, c0:512], op=ALU.add),
                     reads=[k1, 'Rsb'], writes=[ka])
                P.op('dve', lambda E, P2=P2, c0=c0: E.tensor_tensor(out=Rsb[:, c0:512], in0=P2[:, c0:512], in1=Rsb[:, c0:512], op=ALU.add),
                     reads=[k2, 'Rsb'], writes=['Rsb'])
                P.op('act', lambda E, w=w, arg=arg, c0=c0: E.activation(out=w[:, c0:512], in_=arg[:, c0:512], func=AF.Exp),
                     reads=[ka], writes=[kw])
                if diag:
                    P.op('pool', lambda E, w=w, c0=c0: E.tensor_tensor(out=w[:, c0:c0 + 128], in0=w[:, c0:c0 + 128], in1=MASK, op=ALU.mult),
                         reads=[kw, 'cst'], writes=[kw])
                for j in range(j0, 4):
                    P.op('pe', lambda E, acc=acc, w=w, j=j, kb=kb: E.matmul(
                        acc[:, j * 64:(j + 1) * 64], w[:, j * 128:(j + 1) * 128], V[:, kb, :], start=False, stop=(kb == 0)),
                        reads=[kw, 'V'], writes=[ak])
            P.op('act', lambda E, acc=acc, ob=ob: E.activation(out=ob[:, :, :], in_=acc[:, 0:256], func=AF.Copy),
                 reads=[ak], writes=[obk])
            P.dma('sp', o_d[512 * G:512 * (G + 1), :].rearrange("(j p) d -> p j d", p=128), ob[:, :, :], reads=[obk])
        P.finish('sp')
        P.emit()
    return nc


def sb_consts():
    import ml_dtypes
    k = np.arange(128)
    nincl = -(k[:, None] >= k[None, :]).astype(np.float32)
    nones = -np.ones((128, 128), np.float32)
    mask = (k[:, None] < k[None, :]).astype(np.float32)
    return np.stack([nincl, nones, mask], axis=1).astype(ml_dtypes.bfloat16)


def build_fox(S):
    nc = bass.Bass("TRN2", target_bir_lowering=False)
    qT_d = nc.dram_tensor("qT", [64, S], BF16, kind="ExternalInput").ap()
    kT_d = nc.dram_tensor("kT", [64, S], BF16, kind="ExternalInput").ap()
    v_d = nc.dram_tensor("v", [S, 64], BF16, kind="ExternalInput").ap()
    ff_d = nc.dram_tensor("ff", [S], F32, kind="ExternalInput").ap()
    fb_d = nc.dram_tensor("fb", [128, 1], F32, kind="ExternalInput").ap()
    cst_d = nc.dram_tensor("cst", [128, 2, 128], BF16, kind="ExternalInput").ap()
    ut_d = nc.dram_tensor("ut", [128, 128], F32, kind="ExternalInput").ap()
    o_d = nc.dram_tensor("o", [S, 64], F32, kind="ExternalOutput").ap()
    scr = nc.dram_tensor("scr", [6, S], BF16, kind="Internal").ap()
    NB = S // 128
    NG = NB // 4
    NI = S // 128
    with ExitStack() as es:
        P = Prog(nc, es)
        QT = P.sb("QT", [70, S], BF16)
        KT = P.sb("KT", [70, S], BF16)
        V = P.sb("V_sb", [128, NB, 65], BF16)
        cst = P.sb("cst_sb", [128, 2, 128], BF16)
        ut = P.sb("ut_sb", [128, 128], F32)
        zer = P.sb("zer", [128, 512], BF16)
        fb = P.sb("fb_sb", [128, 1], F32)
        nfb = P.sb("nfb", [128, 1], F32)
        f1 = P.sb("f1", [128, NI], F32)
        f2 = P.sb("f2", [128, NI], F32)
        onesf = P.sb("onesf", [128, NI], F32)
        cs = P.sb("cs", [128, NI], F32)
        off = P.sb("off", [128, 1], F32)
        r1 = P.sb("r1", [128, NI], F32)
        parts = P.sb("parts", [128, 6, NI], BF16)
        e_ = [P.sb(f"e{i}", [128, 512], BF16) for i in range(3)]
        ob_ = [P.sb(f"ob{i}", [128, 4, 64], F32) for i in range(2)]
        rl_ = [P.sb(f"rl{i}", [128, 4], F32) for i in range(2)]
        P1_ = [P.ps(f"P1_{i}", [128, 512]) for i in range(3)]
        acc_ = [P.ps(f"acc{i}", [128, 512]) for i in range(2)]
        pso = P.ps("pso", [128, 512])

        nchunk = max(1, S // 4096)
        csz = S // nchunk
        for i in range(nchunk):
            P.dma('sp', QT[0:64, i * csz:(i + 1) * csz], qT_d[:, i * csz:(i + 1) * csz], writes=['QTm'])
            P.dma('sp', KT[0:64, i * csz:(i + 1) * csz], kT_d[:, i * csz:(i + 1) * csz], writes=['KTm'])
        P.dma('sp', V[:, :, 0:64], v_d.rearrange("(n p) d -> p n d", p=128), writes=['Vm'])
        P.op('pool', lambda E: E.memset(V[:, :, 64:65], 1.0), writes=['V1'])
        P.dma('sp', cst[:, :, :], cst_d[:, :, :], writes=['cst'])
        P.dma('sp', ut[:, :], ut_d[:, :], writes=['ut'])
        P.dma('sp', fb[:, :], fb_d[:, :], writes=['fb'])
        P.dma('sp', f1[:, :], ff_d.rearrange("(p i) -> p i", p=128), writes=['f1'])
        P.op('pool', lambda E: E.memset(zer[:, :], 0.0), writes=['zer'])
        P.op('pool', lambda E: E.memset(onesf[:, :], 1.0), writes=['onesf'])
        P.op('pool', lambda E: E.memset(QT[64:70, :], 1.0), writes=['QTa'])
        P.op('pool', lambda E: E.memset(KT[64:70, :], 1.0), writes=['KTa'])
        P.op('dve', lambda E: E.tensor_scalar(out=nfb[:, :], in0=fb[:, :], scalar1=-1.0, scalar2=None, op0=ALU.mult),
             reads=['fb'], writes=['nfb'])
        P.op('act', lambda E: E.activation(out=f2[:, :], in_=f1[:, :], func=AF.Exp, bias=nfb[:, 0:1], scale=-1.0),
             reads=['f1', 'nfb'], writes=['f2'])
        P.op('act', lambda E: E.activation(out=f1[:, :], in_=f2[:, :], func=AF.Ln, bias=1.0, scale=1.0),
             reads=['f2'], writes=['f1'])
        P.op('dve', lambda E: E.tensor_tensor_scan(out=cs[:, :], data0=onesf[:, :], data1=f1[:, :], initial=0.0,
                                                   op0=ALU.mult, op1=ALU.add), reads=['f1', 'onesf'], writes=['cs'])
        P.op('pe', lambda E: E.matmul(pso[:, 0:1], ut[:, :], cs[:, NI - 1:NI], start=True, stop=True),
             reads=['ut', 'cs'], writes=['pso'])
        P.op('dve', lambda E: E.tensor_copy(out=off[:, :], in_=pso[:, 0:1]), reads=['pso'], writes=['off'])
        P.op('dve', lambda E: E.tensor_scalar(out=cs[:, :], in0=cs[:, :], scalar1=off[:, 0:1], scalar2=None, op0=ALU.add),
             reads=['cs', 'off'], writes=['cs'])
        P.op('dve', lambda E: E.tensor_copy(out=parts[:, 3, :], in_=cs[:, :]), reads=['cs'], writes=['p3'])
        P.op('dve', lambda E: E.tensor_tensor(out=r1[:, :], in0=cs[:, :], in1=parts[:, 3, :], op=ALU.subtract),
             reads=['cs', 'p3'], writes=['r1'])
        P.op('dve', lambda E: E.tensor_copy(out=parts[:, 4, :], in_=r1[:, :]), reads=['r1'], writes=['p4'])
        P.op('dve', lambda E: E.tensor_tensor(out=r1[:, :], in0=r1[:, :], in1=parts[:, 4, :], op=ALU.subtract),
             reads=['r1', 'p4'], writes=['r1'])
        P.op('dve', lambda E: E.tensor_copy(out=parts[:, 5, :], in_=r1[:, :]), reads=['r1'], writes=['p5'])
        P.op('dve', lambda E: E.tensor_scalar(out=parts[:, 0:3, :], in0=parts[:, 3:6, :], scalar1=-1.0, scalar2=None, op0=ALU.mult),
             reads=['p3', 'p4', 'p5'], writes=['p012'])
        for r in range(6):
            P.dma('sp', scr[r, :].rearrange("(p i) -> p i", p=128), parts[:, r, :], reads=['p012', 'p3', 'p4', 'p5'], writes=['scr'])
        P.dma('sp', QT[64:67, :], scr[0:3, :], reads=['scr'], writes=['QTa'])
        P.dma('sp', KT[67:70, :], scr[3:6, :], reads=['scr'], writes=['KTa'])
        IDENT = cst[:, 0, :]
        NEGM = cst[:, 1, :]
        u = 0
        for G in range(NG):
            acc = acc_[G % 2]; ak = f"acc{G % 2}"
            ob = ob_[G % 2]; obk = f"ob{G % 2}"
            rl = rl_[G % 2]; rlk = f"rl{G % 2}"
            accv = acc[:, 0:260].rearrange("p (j d) -> p j d", d=65)
            P.op('pe', lambda E, acc=acc: E.matmul(acc[:, 0:260], zer[:, 0:128], zer[:, 0:260], start=True, stop=False),
                 reads=['zer'], writes=[ak])
            for kb in range(4 * G + 3, -1, -1):
                j0 = max(0, kb - 4 * G)
                diag = kb >= 4 * G
                c0 = j0 * 128
                s = u % 3
                u += 1
                P1 = P1_[s]; e = e_[s]
                k1, ke = f"P1{s}", f"e{s}"
                q0 = 512 * G
                P.op('pe', lambda E, P1=P1, kb=kb, c0=c0, q0=q0, diag=diag: E.matmul(
                    P1[:, c0:512], KT[:, kb * 128:(kb + 1) * 128], QT[:, q0 + c0:q0 + 512], start=True, stop=(not diag)),
                    reads=['KTm', 'QTm', 'KTa', 'QTa'], writes=[k1])
                if diag:
                    P.op('pe', lambda E, P1=P1, c0=c0: E.matmul(P1[:, c0:c0 + 128], IDENT, NEGM, start=False, stop=True),
                         reads=['cst'], writes=[k1])
                P.op('act', lambda E, P1=P1, e=e, c0=c0: E.activation(out=e[:, c0:512], in_=P1[:, c0:512], func=AF.Exp),
                     reads=[k1], writes=[ke])
                for j in range(j0, 4):
                    P.op('pe', lambda E, accv=accv, e=e, j=j, kb=kb: E.matmul(
                        accv[:, j, :], e[:, j * 128:(j + 1) * 128], V[:, kb, :], start=False, stop=(kb == 0)),
                        reads=[ke, 'Vm', 'V1'], writes=[ak])
            P.op('dve', lambda E, accv=accv, rl=rl: E.reciprocal(out=rl[:, :], in_=accv[:, :, 64]),
                 reads=[ak], writes=[rlk])
            for j in range(4):
                P.op('dve', lambda E, accv=accv, rl=rl, ob=ob, j=j: E.tensor_scalar(
                    out=ob[:, j, :], in0=accv[:, j, 0:64], scalar1=rl[:, j:j + 1], scalar2=None, op0=ALU.mult),
                    reads=[ak, rlk], writes=[obk])
            P.dma('sp', o_d[512 * G:512 * (G + 1), :].rearrange("(j p) d -> p j d", p=128), ob[:, :, :], reads=[obk])
        P.finish('sp')
        P.emit()
    return nc


def fox_consts():
    import ml_dtypes
    k = np.arange(128)
    ident = np.eye(128, dtype=np.float32)
    negm = np.where(k[:, None] > k[None, :], -30000.0, 0.0).astype(np.float32)
    cst = np.stack([ident, negm], axis=1).astype(ml_dtypes.bfloat16)
    ut = (k[:, None] < k[None, :]).astype(np.float32)
    return cst, ut


def build_swa(S):
    nc = bass.Bass("TRN2", target_bir_lowering=False)
    qT_d = nc.dram_tensor("qT", [64, S], BF16, kind="ExternalInput").ap()
    kT_d = nc.dram_tensor("kT", [64, S], BF16, kind="ExternalInput").ap()
    v_d = nc.dram_tensor("v", [S, 64], BF16, kind="ExternalInput").ap()
    bt_d = nc.dram_tensor("bt", [128, 256], F32, kind="ExternalInput").ap()
    snk_d = nc.dram_tensor("snk", [128, 1], F32, kind="ExternalInput").ap()
    o_d = nc.dram_tensor("o", [S, 64], F32, kind="ExternalOutput").ap()
    NB = S // 128
    with ExitStack() as es:
        P = Prog(nc, es)
        QT = P.sb("QT", [64, S], BF16)
        KT = P.sb("KT", [64, S], BF16)
        V = P.sb("V_sb", [128, NB, 65], BF16)
        BT = P.sb("BT", [128, 256], F32)
        snk = P.sb("snk_sb", [128, 1], F32)
        esk = P.sb("esk", [128, 1], F32)
        arg_ = [P.sb(f"arg{i}", [128, 256], F32) for i in range(2)]
        e_ = [P.sb(f"e{i}", [128, 256], BF16) for i in range(3)]
        ob_ = [P.sb(f"ob{i}", [128, 4, 64], F32) for i in range(2)]
        rl_ = [P.sb(f"rl{i}", [128, 4], F32) for i in range(2)]
        P1_ = [P.ps(f"P1_{i}", [128, 512]) for i in range(2)]
        acc_ = [P.ps(f"acc{i}", [128, 512]) for i in range(2)]
        nchunk = max(1, S // 4096)
        csz = S // nchunk
        for i in range(nchunk):
            P.dma('sp', QT[:, i * csz:(i + 1) * csz], qT_d[:, i * csz:(i + 1) * csz], writes=['QT'])
            P.dma('sp', KT[:, i * csz:(i + 1) * csz], kT_d[:, i * csz:(i + 1) * csz], writes=['KT'])
        P.dma('sp', V[:, :, 0:64], v_d.rearrange("(n p) d -> p n d", p=128), writes=['Vm'])
        P.op('pool', lambda E: E.memset(V[:, :, 64:65], 1.0), writes=['V1'])
        P.dma('sp', BT[:, :], bt_d[:, :], writes=['BT'])
        P.dma('sp', snk[:, :], snk_d[:, :], writes=['snk'])
        P.op('act', lambda E: E.activation(out=esk[:, :], in_=snk[:, :], func=AF.Exp), reads=['snk'], writes=['esk'])
        for m in range(NB):
            ncol = 256 if m < NB - 1 else 128
            s2 = m % 2; s3 = m % 3
            P1 = P1_[s2]; arg = arg_[s2]; e = e_[s3]
            G = m // 4; j = m % 4
            acc = acc_[G % 2]; ak = f"acc{G % 2}"
            accv = acc[:, 0:260].rearrange("p (j d) -> p j d", d=65)
            P.op('pe', lambda E, P1=P1, m=m, ncol=ncol: E.matmul(
                P1[:, 0:ncol], KT[:, m * 128:(m + 1) * 128], QT[:, m * 128:m * 128 + ncol], start=True, stop=True),
                reads=['KT', 'QT'], writes=[f"P1{s2}"])
            P.op('dve', lambda E, P1=P1, arg=arg, ncol=ncol: E.tensor_tensor(
                out=arg[:, 0:ncol], in0=P1[:, 0:ncol], in1=BT[:, 0:ncol], op=ALU.add),
                reads=[f"P1{s2}", 'BT'], writes=[f"arg{s2}"])
            P.op('act', lambda E, arg=arg, e=e, ncol=ncol: E.activation(out=e[:, 0:ncol], in_=arg[:, 0:ncol], func=AF.Exp),
                 reads=[f"arg{s2}"], writes=[f"e{s3}"])
            if m > 0:
                ep = e_[(m - 1) % 3]
                P.op('pe', lambda E, accv=accv, ep=ep, m=m, j=j: E.matmul(
                    accv[:, j, :], ep[:, 128:256], V[:, m - 1, :], start=True, stop=False),
                    reads=[f"e{(m - 1) % 3}", 'Vm', 'V1'], writes=[ak])
            P.op('pe', lambda E, accv=accv, e=e, m=m, j=j: E.matmul(
                accv[:, j, :], e[:, 0:128], V[:, m, :], start=(m == 0), stop=True),
                reads=[f"e{s3}", 'Vm', 'V1'], writes=[ak])
            if j == 3:
                ob = ob_[G % 2]; obk = f"ob{G % 2}"
                rl = rl_[G % 2]; rlk = f"rl{G % 2}"
                P.op('dve', lambda E, accv=accv, rl=rl: E.tensor_scalar(
                    out=rl[:, :], in0=accv[:, :, 64], scalar1=esk[:, 0:1], scalar2=None, op0=ALU.add),
                    reads=[ak, 'esk'], writes=[rlk])
                P.op('dve', lambda E, rl=rl: E.reciprocal(out=rl[:, :], in_=rl[:, :]), reads=[rlk], writes=[rlk])
                for jj in range(4):
                    P.op('dve', lambda E, accv=accv, rl=rl, ob=ob, jj=jj: E.tensor_scalar(
                        out=ob[:, jj, :], in0=accv[:, jj, 0:64], scalar1=rl[:, jj:jj + 1], scalar2=None, op0=ALU.mult),
                        reads=[ak, rlk], writes=[obk])
                P.dma('sp', o_d[512 * G:512 * (G + 1), :].rearrange("(j p) d -> p j d", p=128), ob[:, :, :], reads=[obk])
        P.finish('sp')
        P.emit()
    return nc


def t5_buckets(dist):
    import math
    max_exact = 16
    safe = np.maximum(dist, 1).astype(np.float32)
    large = max_exact + (np.log(safe / max_exact) / math.log(128 / max_exact) * (32 - max_exact)).astype(np.int32)
    large = np.minimum(large, 31)
    return np.where(dist < max_exact, dist, large).astype(np.int32)


def swa_bias_T(rel_col):
    kl = np.arange(128)[:, None]
    qc = np.arange(256)[None, :]
    dist = qc - kl
    valid = (dist >= 0) & (dist < 128)
    bucket = t5_buckets(np.clip(dist, 0, None))
    bt = rel_col[bucket].astype(np.float32)
    bt[~valid] = -30000.0
    return bt


I32 = mybir.dt.int32
TWO_PI = 6.283185307179586


def build_ssm(S):
    L = 16
    NC = S // L
    NLV = int(np.log2(NC))
    assert 2 ** NLV == NC
    CW = min(512, NC)
    NCH = NC // CW
    nc = bass.Bass("TRN2", target_bir_lowering=False)
    uT_d = nc.dram_tensor("uT", [64, S], BF16, kind="ExternalInput").ap()
    scal_d = nc.dram_tensor("scal", [128, 3, 4], F32, kind="ExternalInput").ap()
    sgn_d = nc.dram_tensor("sgn", [128, 1], F32, kind="ExternalInput").ap()
    BS_d = nc.dram_tensor("BS", [128, 2, 4, 64], F32, kind="ExternalInput").ap()
    CS_d = nc.dram_tensor("CS1", [128, 4, 64], F32, kind="ExternalInput").ap()
    Dd_d = nc.dram_tensor("Dd", [64, 64], F32, kind="ExternalInput").ap()
    mats_d = nc.dram_tensor("mats", [128, 2, 128], F32, kind="ExternalInput").ap()
    yT_d = nc.dram_tensor("yT", [64, S], F32, kind="ExternalOutput").ap()
    with ExitStack() as es:
        P = Prog(nc, es)
        uT = P.sb("uT_sb", [64, S], BF16)
        scal = P.sb("scal_sb", [128, 3, 4], F32)
        sgn = P.sb("sgn_sb", [128, 1], F32)
        BS = P.sb("BS_sb", [128, 2, 4, 64], F32)
        CS1 = P.sb("CS_sb", [128, 4, 64], F32)
        Dd = P.sb("Dd_sb", [64, 64], F32)
        mats = P.sb("mats_sb", [128, 2, 128], F32)
        IDENT = mats[:, 0, :]
        SWAP = mats[:, 1, :]
        nchunk = max(1, S // 4096)
        csz = S // nchunk
        for i in range(nchunk):
            P.dma('sp', uT[:, i * csz:(i + 1) * csz], uT_d[:, i * csz:(i + 1) * csz], writes=['uT'])
        P.dma('sp', scal[:, :, :], scal_d[:, :, :], writes=['scal'])
        P.dma('sp', sgn[:, :], sgn_d[:, :], writes=['sgn'])
        P.dma('sp', BS[:, :, :, :], BS_d[:, :, :, :], writes=['BS'])
        P.dma('sp', CS1[:, :, :], CS_d[:, :, :], writes=['CS1'])
        P.dma('sp', Dd[:, :], Dd_d[:, :], writes=['Dd'])
        P.dma('sp', mats[:, :, :], mats_d[:, :, :], writes=['mats'])
        names = ['dt', 'mag', 'ang', 't1', 'hf', 'ah', 'sn', 'cs', 'sr', 'cr_', 'ar', 'ai', 'den', 'nr',
                 'cre', 'cim', 't2', 's2', 'ns2', 'tci']
        T_ = {n: P.sb("s_" + n, [128, 4], F32) for n in names}
        ti = P.sb("s_ti", [128, 4], I32)
        lamr = scal[:, 0, :]
        lami = scal[:, 1, :]

        def Vv(fn, r, w):
            P.op('dve', fn, reads=r, writes=w)

        def Aa(fn, r, w):
            P.op('act', fn, reads=r, writes=w)

        def tt(o, a, b, op):
            Vv(lambda E: E.tensor_tensor(out=T_[o][:, :], in0=(T_[a][:, :] if isinstance(a, str) else a),
                                         in1=(T_[b][:, :] if isinstance(b, str) else b), op=op),
               [a if isinstance(a, str) else 'scal', b if isinstance(b, str) else 'scal'], [o])

        def tsc(o, a, s1, op0, s2=None, op1=None):
            if op1 is None:
                Vv(lambda E: E.tensor_scalar(out=T_[o][:, :], in0=T_[a][:, :], scalar1=s1, scalar2=None, op0=op0), [a, 'sgn'], [o])
            else:
                Vv(lambda E: E.tensor_scalar(out=T_[o][:, :], in0=T_[a][:, :], scalar1=s1, scalar2=s2, op0=op0, op1=op1), [a, 'sgn'], [o])

        Aa(lambda E: E.activation(out=T_['dt'][:, :], in_=scal[:, 2, :], func=AF.Exp), ['scal'], ['dt'])
        tt('t1', lamr, 'dt', ALU.mult)
        Aa(lambda E: E.activation(out=T_['mag'][:, :], in_=T_['t1'][:, :], func=AF.Exp), ['t1'], ['mag'])
        tt('ang', lami, 'dt', ALU.mult)
        tsc('t1', 'ang', 1.0 / TWO_PI, ALU.mult)
        Vv(lambda E: E.tensor_copy(out=ti[:, :], in_=T_['t1'][:, :]), ['t1'], ['ti'])
        Vv(lambda E: E.tensor_copy(out=T_['t1'][:, :], in_=ti[:, :]), ['ti'], ['t1'])
        Vv(lambda E: E.scalar_tensor_tensor(out=T_['hf'][:, :], in0=T_['t1'][:, :], scalar=-TWO_PI, in1=T_['ang'][:, :],
                                            op0=ALU.mult, op1=ALU.add), ['t1', 'ang'], ['hf'])
        tsc('hf', 'hf', 0.5, ALU.mult)
        Aa(lambda E: E.activation(out=T_['sn'][:, :], in_=T_['hf'][:, :], func=AF.Sin), ['hf'], ['sn'])
        Aa(lambda E: E.activation(out=T_['ah'][:, :], in_=T_['hf'][:, :], func=AF.Sin, scale=0.5), ['hf'], ['ah'])
        tt('cs', 'ah', 'ah', ALU.mult)
        tsc('cs', 'cs', -2.0, ALU.mult, 1.0, ALU.add)
        tt('sr', 'sn', 'cs', ALU.mult)
        tsc('sr', 'sr', 2.0, ALU.mult)
        tt('cr_', 'sn', 'sn', ALU.mult)
        tsc('cr_', 'cr_', -2.0, ALU.mult, 1.0, ALU.add)
        tt('ar', 'mag', 'cr_', ALU.mult)
        tt('ai', 'mag', 'sr', ALU.mult)
        tt('den', lamr, lamr, ALU.mult)
        tt('t1', lami, lami, ALU.mult)
        tt('den', 'den', 't1', ALU.add)
        Vv(lambda E: E.reciprocal(out=T_['den'][:, :], in_=T_['den'][:, :]), ['den'], ['den'])
        tsc('nr', 'ar', -1.0, ALU.add)
        tt('t1', 'nr', lamr, ALU.mult)
        tt('t2', 'ai', lami, ALU.mult)
        tt('t1', 't1', 't2', ALU.add)
        tt('cre', 't1', 'den', ALU.mult)
        tt('t1', 'ai', lamr, ALU.mult)
        tt('t2', 'nr', lami, ALU.mult)
        tt('t1', 't1', 't2', ALU.subtract)
        tt('cim', 't1', 'den', ALU.mult)
        tsc('s2', 'ai', sgn[:, 0:1], ALU.mult)
        tsc('ns2', 's2', -1.0, ALU.mult)
        tsc('tci', 'cim', sgn[:, 0:1], ALU.mult, -1.0, ALU.mult)

        A0 = [P.sb(f"A0_{g}", [128, 128], F32) for g in range(4)]
        B0 = [P.sb(f"B0_{g}", [128, 128], F32) for g in range(4)]
        Bm = [P.sb(f"Bm_{g}", [128, 64], F32) for g in range(4)]
        WcA = [P.sb(f"Wc_{g}", [128, 16, 64], F32) for g in range(4)]
        UA = [P.sb(f"UA_{g}", [128, 17, 64], F32) for g in range(4)]
        WIN = [P.sb(f"WIN_{g}", [64, 16, 128], BF16) for g in range(4)]
        WOUT = [P.sb(f"WOUT_{g}", [128, 16, 64], BF16) for g in range(4)]
        KernT = P.sb("KernT", [64, 16, 64], BF16)
        Ak = [[P.sb(f"Ak_{g}_{i}", [128, 128], F32) for i in range(2)] for g in range(4)]
        Bk = [[P.sb(f"Bk_{g}_{i}", [128, 128], F32) for i in range(2)] for g in range(4)]
        ALV = [P.sb(f"ALV_{g}", [128, max(NLV, 1), 128], F32) for g in range(4)]
        pw = [P.ps(f"pw{i}", [128, 512]) for i in range(4)]
        pk = [P.ps(f"pk{i}", [128, 512]) for i in range(2)]
        py = [P.ps(f"py{i}", [128, 512]) for i in range(2)]
        npw = [0]

        def nextpw():
            i = npw[0] % 4
            npw[0] += 1
            return pw[i], f"pw{i}"

        for g in range(4):
            Vv(lambda E, g=g: E.tensor_scalar(out=A0[g][:, :], in0=IDENT, scalar1=T_['ar'][:, g:g + 1], scalar2=None, op0=ALU.mult),
               ['mats', 'ar'], [f'A0{g}'])
            Vv(lambda E, g=g: E.scalar_tensor_tensor(out=B0[g][:, :], in0=SWAP, scalar=T_['ns2'][:, g:g + 1], in1=A0[g][:, :],
                                                     op0=ALU.mult, op1=ALU.add), ['mats', 'ns2', f'A0{g}'], [f'B0{g}'])
            Vv(lambda E, g=g: E.scalar_tensor_tensor(out=A0[g][:, :], in0=SWAP, scalar=T_['s2'][:, g:g + 1], in1=A0[g][:, :],
                                                     op0=ALU.mult, op1=ALU.add), ['mats', 's2', f'A0{g}', f'B0{g}'], [f'A0{g}'])
            Vv(lambda E, g=g: E.tensor_scalar(out=Bm[g][:, :], in0=BS[:, 0, g, :], scalar1=T_['cre'][:, g:g + 1], scalar2=None, op0=ALU.mult),
               ['BS', 'cre'], [f'Bm{g}'])
            Vv(lambda E, g=g: E.scalar_tensor_tensor(out=Bm[g][:, :], in0=BS[:, 1, g, :], scalar=T_['tci'][:, g:g + 1], in1=Bm[g][:, :],
                                                     op0=ALU.mult, op1=ALU.add), ['BS', 'tci', f'Bm{g}'], [f'Bm{g}'])
            Vv(lambda E, g=g: E.tensor_copy(out=WcA[g][:, 15, :], in_=Bm[g][:, :]), [f'Bm{g}'], [f'Wc{g}_15'])
            Vv(lambda E, g=g: E.tensor_scalar(out=UA[g][:, 0, :], in0=CS1[:, g, :], scalar1=sgn[:, 0:1], scalar2=None, op0=ALU.mult),
               ['CS1', 'sgn'], [f'U{g}_0'])
        for step in range(16):
            for g in range(4):
                m = step
                pb, pbk = nextpw()
                P.op('pe', lambda E, g=g, m=m, pb=pb: E.matmul(pb[:, 0:64], B0[g][:, :], UA[g][:, m, :], start=True, stop=True),
                     reads=[f'B0{g}', f'U{g}_{m}'], writes=[pbk])
                P.op('act', lambda E, g=g, m=m, pb=pb: E.activation(out=UA[g][:, m + 1, :], in_=pb[:, 0:64], func=AF.Copy),
                     reads=[pbk], writes=[f'U{g}_{m + 1}'])
                sg = 15 - step
                pb, pbk = nextpw()
                P.op('pe', lambda E, g=g, sg=sg, pb=pb: E.matmul(pb[0:64, 0:128], WcA[g][:, sg, :], IDENT, start=True, stop=True),
                     reads=[f'Wc{g}_{sg}', 'mats'], writes=[pbk])
                P.op('dve', lambda E, g=g, sg=sg, pb=pb: E.tensor_copy(out=WIN[g][:, sg, :], in_=pb[0:64, 0:128]),
                     reads=[pbk], writes=[f'WIN{g}'])
                if sg > 0:
                    pb, pbk = nextpw()
                    P.op('pe', lambda E, g=g, sg=sg, pb=pb: E.matmul(pb[:, 0:64], A0[g][:, :], WcA[g][:, sg, :], start=True, stop=True),
                         reads=[f'A0{g}', f'Wc{g}_{sg}'], writes=[pbk])
                    P.op('act', lambda E, g=g, sg=sg, pb=pb: E.activation(out=WcA[g][:, sg - 1, :], in_=pb[:, 0:64], func=AF.Copy),
                         reads=[pbk], writes=[f'Wc{g}_{sg - 1}'])
        for g in range(4):
            P.op('dve', lambda E, g=g: E.tensor_copy(out=WOUT[g][:, :, :], in_=UA[g][:, 1:17, :]),
                 reads=[f'U{g}_{m}' for m in range(1, 17)], writes=[f'WOUT{g}'])
        for half in range(2):
            for g in range(4):
                P.op('pe', lambda E, g=g, half=half: E.matmul(
                    pk[half][0:64, :], Bm[g][:, :], UA[g][:, half * 8:half * 8 + 8, :], start=(g == 0), stop=(g == 3)),
                    reads=[f'Bm{g}'] + [f'U{g}_{m}' for m in range(16)], writes=[f'pk{half}'])
        P.op('dve', lambda E: E.tensor_tensor(out=KernT[:, 0, :], in0=pk[0][0:64, 0:64], in1=Dd[:, :], op=ALU.add),
             reads=['pk0', 'Dd'], writes=['KernT0'])
        P.op('dve', lambda E: E.tensor_copy(out=KernT[:, 1:8, :], in_=pk[0][0:64, 64:512]), reads=['pk0'], writes=['KernT1'])
        P.op('act', lambda E: E.activation(out=KernT[:, 8:16, :], in_=pk[1][0:64, :], func=AF.Copy), reads=['pk1'], writes=['KernT2'])
        KR = ['KernT0', 'KernT1', 'KernT2']
        cur = [(A0[g], B0[g], f'A0{g}', f'B0{g}') for g in range(4)]
        nsq = 4 + max(NLV - 1, 0)
        for sq in range(nsq):
            for g in range(4):
                A_, B_, ka, kb_ = cur[g]
                An, Bn = Ak[g][sq % 2], Bk[g][sq % 2]
                kan, kbn = f'Ak{g}_{sq % 2}', f'Bk{g}_{sq % 2}'
                pb, pbk = nextpw()
                P.op('pe', lambda E, A_=A_, B_=B_, pb=pb: E.matmul(pb[:, 0:128], B_[:, :], A_[:, :], start=True, stop=True),
                     reads=[ka, kb_], writes=[pbk])
                P.op('act', lambda E, An=An, pb=pb: E.activation(out=An[:, :], in_=pb[:, 0:128], func=AF.Copy),
                     reads=[pbk], writes=[kan])
                pb2, pbk2 = nextpw()
                P.op('pe', lambda E, A_=A_, B_=B_, pb2=pb2: E.matmul(pb2[:, 0:128], A_[:, :], B_[:, :], start=True, stop=True),
                     reads=[ka, kb_], writes=[pbk2])
                P.op('dve', lambda E, Bn=Bn, pb2=pb2: E.tensor_copy(out=Bn[:, :], in_=pb2[:, 0:128]),
                     reads=[pbk2], writes=[kbn])
                cur[g] = (An, Bn, kan, kbn)
                if sq >= 3:
                    lv = sq - 3
                    if lv < NLV:
                        P.op('pool', lambda E, g=g, lv=lv, An=An: E.tensor_copy(out=ALV[g][:, lv, :], in_=An[:, :]),
                             reads=[kan], writes=[f'ALV{g}_{lv}'])
        uv = uT[:, :].rearrange("p (c s) -> p s c", s=16)
        Xa = [P.sb(f"Xa_{g}", [128, NC], F32) for g in range(4)]
        Xb = [P.sb(f"Xb_{g}", [128, NC], F32) for g in range(4)]
        Xp = [P.sb(f"Xp_{g}", [128, NC], BF16) for g in range(4)]
        for g in range(4):
            for ch in range(NCH):
                pb, pbk = nextpw()
                for s_ in range(16):
                    P.op('pe', lambda E, g=g, ch=ch, s_=s_, pb=pb: E.matmul(
                        pb[:, 0:CW], WIN[g][:, s_, :], uv[:, s_, ch * CW:(ch + 1) * CW], start=(s_ == 0), stop=(s_ == 15)),
                        reads=[f'WIN{g}', 'uT'], writes=[pbk])
                P.op('act', lambda E, g=g, ch=ch, pb=pb: E.activation(out=Xa[g][:, ch * CW:(ch + 1) * CW], in_=pb[:, 0:CW], func=AF.Copy),
                     reads=[pbk], writes=[f'Xa{g}'])
        src = [(Xa[g], f'Xa{g}') for g in range(4)]
        dst = [(Xb[g], f'Xb{g}') for g in range(4)]
        for lv in range(NLV):
            d = 2 ** lv
            for g in range(4):
                Xs, ks = src[g]
                Xd, kd = dst[g]
                P.op('pool', lambda E, Xs=Xs, Xd=Xd, d=d: E.tensor_copy(out=Xd[:, 0:d], in_=Xs[:, 0:d]), reads=[ks], writes=[kd])
                c = d
                while c < NC:
                    n = min(512, NC - c)
                    pb, pbk = nextpw()
                    P.op('pe', lambda E, g=g, lv=lv, Xs=Xs, c=c, n=n, d=d, pb=pb: E.matmul(
                        pb[:, 0:n], ALV[g][:, lv, :], Xs[:, c - d:c - d + n], start=True, stop=True),
                        reads=[f'ALV{g}_{lv}', ks], writes=[pbk])
                    P.op('dve', lambda E, Xs=Xs, Xd=Xd, c=c, n=n, pb=pb: E.tensor_tensor(
                        out=Xd[:, c:c + n], in0=pb[:, 0:n], in1=Xs[:, c:c + n], op=ALU.add), reads=[pbk, ks], writes=[kd])
                    c += n
            src, dst = dst, src
        for g in range(4):
            Xs, ks = src[g]
            P.op('pool', lambda E, g=g: E.memset(Xp[g][:, 0:1], 0.0), writes=[f'Xp{g}'])
            P.op('pool', lambda E, g=g, Xs=Xs: E.tensor_copy(out=Xp[g][:, 1:NC], in_=Xs[:, 0:NC - 1]), reads=[ks], writes=[f'Xp{g}'])
        OW = min(256, NC)
        yst = P.sb("yst", [64, OW, 16], F32)
        g1 = P.sb("g1", [64, OW * 16], F32)
        for ch in range(NC // OW):
            ys = yst; yk = "yst"
            for tau in range(16):
                pb = py[tau % 2]; pbk = f"py{tau % 2}"
                for g in range(4):
                    P.op('pe', lambda E, g=g, tau=tau, ch=ch, pb=pb: E.matmul(
                        pb[0:64, 0:OW], WOUT[g][:, tau, :], Xp[g][:, ch * OW:(ch + 1) * OW], start=(g == 0), stop=False),
                        reads=[f'WOUT{g}', f'Xp{g}'], writes=[pbk])
                for s_ in range(tau + 1):
                    P.op('pe', lambda E, s_=s_, tau=tau, ch=ch, pb=pb: E.matmul(
                        pb[0:64, 0:OW], KernT[:, tau - s_, :], uv[:, s_, ch * OW:(ch + 1) * OW], start=False, stop=(s_ == tau)),
                        reads=KR + ['uT'], writes=[pbk])
                if tau % 2 == 0:
                    P.op('act', lambda E, ys=ys, tau=tau, pb=pb: E.activation(out=ys[:, :, tau], in_=pb[0:64, 0:OW], func=AF.Copy),
                         reads=[pbk], writes=[yk])
                else:
                    P.op('dve', lambda E, ys=ys, tau=tau, pb=pb: E.tensor_copy(out=ys[:, :, tau], in_=pb[0:64, 0:OW]),
                         reads=[pbk], writes=[yk])
            yf = ys[:, :, :].rearrange("p c s -> p (c s)")
            P.op('dve', lambda E, yf=yf: E.tensor_tensor(out=g1[:, :], in0=yf, in1=yf, op=ALU.mult), reads=[yk], writes=['g1'])
            P.op('dve', lambda E: E.tensor_scalar(out=g1[:, :], in0=g1[:, :], scalar1=0.044715, scalar2=1.0, op0=ALU.mult, op1=ALU.add),
                 reads=['g1'], writes=['g1'])
            P.op('dve', lambda E, yf=yf: E.tensor_tensor(out=g1[:, :], in0=g1[:, :], in1=yf, op=ALU.mult), reads=['g1', yk], writes=['g1'])
            P.op('act', lambda E: E.activation(out=g1[:, :], in_=g1[:, :], func=AF.Sigmoid, scale=1.5957691216057308),
                 reads=['g1'], writes=['g1'])
            P.op('dve', lambda E, yf=yf: E.tensor_tensor(out=yf, in0=g1[:, :], in1=yf, op=ALU.mult), reads=['g1', yk], writes=[yk])
            P.dma('sp', yT_d[:, ch * OW * 16:(ch + 1) * OW * 16], yf, reads=[yk])
        P.finish('sp')
        P.emit()
    return nc


def ssm_inputs(lam_re, lam_im, log_dt, b_re, b_im, c_re, c_im, d_skip, j):
    scal = np.zeros((128, 3, 4), np.float32)
    BS = np.zeros((128, 2, 4, 64), np.float32)
    CS1 = np.zeros((128, 4, 64), np.float32)
    Dd = np.zeros((64, 64), np.float32)
    for g in range(4):
        G = 4 * j + g
        scal[0:64, 0, g] = lam_re[G]; scal[64:128, 0, g] = lam_re[G]
        scal[0:64, 1, g] = lam_im[G]; scal[64:128, 1, g] = lam_im[G]
        scal[:, 2, g] = log_dt[G]
        BS[0:64, 0, g, g * 16:(g + 1) * 16] = b_re[G]; BS[64:128, 0, g, g * 16:(g + 1) * 16] = b_im[G]
        BS[0:64, 1, g, g * 16:(g + 1) * 16] = b_im[G]; BS[64:128, 1, g, g * 16:(g + 1) * 16] = b_re[G]
        CS1[0:64, g, g * 16:(g + 1) * 16] = c_re[G].T; CS1[64:128, g, g * 16:(g + 1) * 16] = c_im[G].T
        Dd[np.arange(g * 16, (g + 1) * 16), np.arange(g * 16, (g + 1) * 16)] = d_skip[G]
    sgn = np.ones((128, 1), np.float32); sgn[64:] = -1.0
    k = np.arange(128)
    ident = np.eye(128, dtype=np.float32)
    swap = (k[None, :] == ((k[:, None] + 64) % 128)).astype(np.float32)
    mats = np.stack([ident, swap], axis=1)
    return {"scal": scal, "sgn": sgn, "BS": BS, "CS1": CS1, "Dd": Dd, "mats": mats}


def build_stageM():
    nc = bass.Bass("TRN2", target_bir_lowering=False)
    cT_d = nc.dram_tensor("cT", [128, 8], F32, kind="ExternalInput").ap()
    w_d = nc.dram_tensor("wada", [1024, 6144], F32, kind="ExternalInput").ap()
    b_d = nc.dram_tensor("bada", [128, 48], F32, kind="ExternalInput").ap()
    o_d = nc.dram_tensor("modv", [128, 48], F32, kind="ExternalOutput").ap()
    with ExitStack() as es:
        P = Prog(nc, es)
        cT = P.sb("cT_sb", [128, 8], F32)
        sg = P.sb("sg", [128, 8], F32)
        ca = P.sb("ca", [128, 8], F32)
        bsb = P.sb("b_sb", [128, 48], F32)
        osb = P.sb("o_sb", [128, 48], F32)
        Wb = [P.sb(f"Wb{i}", [128, 8, 768], F32) for i in range(2)]
        ps = P.ps("ps", [128, 512])
        wv = w_d.rearrange("(c p) n -> p c n", p=128)
        P.dma('sp', cT[:, :], cT_d[:, :], writes=['cT'])
        P.dma('sp', bsb[:, :], b_d[:, :], writes=['b'])
        P.op('act', lambda E: E.activation(out=sg[:, :], in_=cT[:, :], func=AF.Sigmoid), reads=['cT'], writes=['sg'])
        P.op('dve', lambda E: E.tensor_tensor(out=ca[:, :], in0=cT[:, :], in1=sg[:, :], op=ALU.mult), reads=['cT', 'sg'], writes=['ca'])
        for grp in range(8):
            W = Wb[grp % 2]; wk = f"W{grp % 2}"
            for c in range(8):
                P.dma('sp', W[:, c, :], wv[:, c, grp * 768:(grp + 1) * 768], writes=[wk])
            for cc in range(6):
                col = grp * 6 + cc
                for c in range(8):
                    P.op('pe', lambda E, W=W, c=c, cc=cc, col=col: E.matmul(
                        ps[:, col:col + 1], W[:, c, cc * 128:(cc + 1) * 128], ca[:, c:c + 1], start=(c == 0), stop=(c == 7)),
                        reads=[wk, 'ca'], writes=['ps'])
        P.op('dve', lambda E: E.tensor_tensor(out=osb[:, :], in0=ps[:, 0:48], in1=bsb[:, :], op=ALU.add), reads=['ps', 'b'], writes=['o'])
        P.dma('sp', o_d[:, :], osb[:, :], reads=['o'])
        P.finish('sp')
        P.emit()
    return nc


def build_stageC(T, final, TT=256):
    nc = bass.Bass("TRN2", target_bir_lowering=False)
    xT = nc.dram_tensor("xT", [D, T], F32, kind="ExternalInput").ap()
    mixT = nc.dram_tensor("mixT", [D, T], F32, kind="ExternalInput").ap()
    wglu_d = nc.dram_tensor("wglu", [256, 256], F32, kind="ExternalInput").ap()
    bglu_d = nc.dram_tensor("bglu", [128, 2], F32, kind="ExternalInput").ap()
    wout_d = nc.dram_tensor("wout", [D, D], F32, kind="ExternalInput").ap()
    w1_d = nc.dram_tensor("w1", [D, 4096], F32, kind="ExternalInput").ap()
    w2_d = nc.dram_tensor("w2", [4096, D], F32, kind="ExternalInput").ap()
    vec_d = nc.dram_tensor("vec", [128, 7, 8], F32, kind="ExternalInput").ap()
    xoT = nc.dram_tensor("xoT", [D, T], F32, kind="ExternalOutput").ap()
    NT = T // TT
    with ExitStack() as es:
        P = Prog(nc, es)
        W1 = P.sb("W1", [128, 8, 4096], BF16)
        W2 = P.sb("W2", [128, 32, 1024], BF16)
        WO = P.sb("WO", [128, 8, 1024], BF16)
        WG = P.sb("WG", [128, 2, 256], BF16)
        bglu = P.sb("bglu_sb", [128, 2], F32)
        vec = P.sb("vec_sb", [128, 7, 8], F32)
        gsc = P.sb("gsc", [128, 8], F32)
        ones_bf = P.sb("ones_bf", [128, 128], BF16)
        X = P.sb("X", [128, 8, TT], F32)
        MIX = P.sb("MIX", [128, 8, TT], F32)
        sq = P.sb("sq", [128, 8, TT], BF16)
        ybf = P.sb("ybf", [128, 2, TT], BF16)
        gate = P.sb("gate", [128, TT], F32)
        rstd = P.sb("rstd", [128, TT], F32)
        mixn = P.sb("mixn", [128, 8, TT], BF16)
        hT = P.sb("hT", [128, 8, TT], BF16)
        aT = P.sb("aT", [128, 32, TT], BF16)
        tmp = [P.sb(f"tmp{i}", [128, TT], F32) for i in range(2)]
        ps_ss = P.ps("ps_ss", [128, 512])
        ps_c = [P.ps(f"ps_c{i}", [128, 512]) for i in range(4)]
        xv = xT.rearrange("(c p) t -> p c t", p=128)
        mv_ = mixT.rearrange("(c p) t -> p c t", p=128)
        ov = xoT.rearrange("(c p) t -> p c t", p=128)
        P.dma('sp', vec[:, :, :], vec_d[:, :, :], writes=['vec'])
        P.dma('sp', bglu[:, :], bglu_d[:, :], writes=['bglu'])
        P.dma('pool', WG[:, :, :], wglu_d.rearrange("(c p) n -> p c n", p=128), writes=['WG'])
        wov = wout_d.rearrange("(c p) n -> p c n", p=128)
        w1v = w1_d.rearrange("(c p) n -> p c n", p=128)
        w2v = w2_d.rearrange("(c p) n -> p c n", p=128)
        for c in range(8):
            P.dma('pool', WO[:, c, :], wov[:, c, :], writes=['WO'])
        for c in range(8):
            P.dma('pool', W1[:, c, :], w1v[:, c, :], writes=['W1'])
        for c in range(32):
            P.dma('pool', W2[:, c, :], w2v[:, c, :], writes=['W2'])
        P.op('dve', lambda E: E.memset(ones_bf[:, :], 1.0), writes=['ones'])
        P.op('dve', lambda E: E.scalar_tensor_tensor(out=gsc[:, :], in0=vec[:, 3, :], scalar=1.0, in1=vec[:, 2, :],
                                                     op0=ALU.add, op1=ALU.mult), reads=['vec'], writes=['gsc'])
        nps = [0]

        def nextps():
            i = nps[0] % 4
            nps[0] += 1
            return ps_c[i], f"psc{i}"

        def stats(src_ap, chunks, inv_n, rk):
            for i, c in enumerate(chunks):
                P.op('pe', lambda E, c=c, i=i: E.matmul(ps_ss[:, 0:TT], ones_bf[:, :], sq[:, c, :],
                                                        start=(i == 0), stop=(i == len(chunks) - 1)),
                     reads=['sq', 'ones'], writes=['ss'])
            P.op('act', lambda E: E.activation(out=rstd[:, :], in_=ps_ss[:, 0:TT], func=AF.Sqrt, bias=NORM_EPS, scale=inv_n),
                 reads=['ss'], writes=['rstd'])
            P.op('dve', lambda E: E.reciprocal(out=rstd[:, :], in_=rstd[:, :]), reads=['rstd'], writes=['rstd'])

        for it in range(NT):
            t0 = it * TT
            P.dma('sp', X[:, :, :], xv[:, :, t0:t0 + TT], writes=['X'])
            P.dma('sp', MIX[:, :, :], mv_[:, :, t0:t0 + TT], writes=['MIX'])
            P.op('act', lambda E: E.activation(out=ybf[:, :, :], in_=MIX[:, 6:8, :], func=AF.Copy), reads=['MIX'], writes=['ybf'])
            for oc in range(2):
                pb, pk = nextps()
                for kc in range(2):
                    P.op('pe', lambda E, oc=oc, kc=kc, pb=pb: E.matmul(pb[:, 0:TT], WG[:, kc, oc * 128:(oc + 1) * 128], ybf[:, kc, :],
                                                                       start=(kc == 0), stop=(kc == 1)), reads=['WG', 'ybf'], writes=[pk])
                P.op('act', lambda E, oc=oc, pb=pb: E.activation(out=gate[:, :], in_=pb[:, 0:TT], func=AF.Sigmoid,
                                                                 bias=bglu[:, oc:oc + 1], scale=1.0), reads=[pk, 'bglu'], writes=['gate'])
                P.op('dve', lambda E, oc=oc: E.tensor_tensor(out=MIX[:, 6 + oc, :], in0=MIX[:, 6 + oc, :], in1=gate[:, :], op=ALU.mult),
                     reads=['MIX', 'gate'], writes=['MIX'])
            P.op('act', lambda E: E.activation(out=sq[:, :, :], in_=MIX[:, :, :], func=AF.Square), reads=['MIX'], writes=['sq'])
            for grp in range(4):
                stats(None, [2 * grp, 2 * grp + 1], 1.0 / 256, None)
                for c in (2 * grp, 2 * grp + 1):
                    P.op('dve', lambda E, c=c: E.scalar_tensor_tensor(out=mixn[:, c, :], in0=MIX[:, c, :], scalar=vec[:, 0, c:c + 1],
                                                                      in1=rstd[:, :], op0=ALU.mult, op1=ALU.mult),
                         reads=['MIX', 'vec', 'rstd'], writes=['mixn'])
            for oc in range(8):
                pb, pk = nextps()
                for kc in range(8):
                    P.op('pe', lambda E, oc=oc, kc=kc, pb=pb: E.matmul(pb[:, 0:TT], WO[:, kc, oc * 128:(oc + 1) * 128], mixn[:, kc, :],
                                                                       start=(kc == 0), stop=(kc == 7)), reads=['WO', 'mixn'], writes=[pk])
                P.op('dve', lambda E, oc=oc, pb=pb: E.scalar_tensor_tensor(out=X[:, oc, :], in0=pb[:, 0:TT], scalar=vec[:, 1, oc:oc + 1],
                                                                           in1=X[:, oc, :], op0=ALU.mult, op1=ALU.add),
                     reads=[pk, 'vec', 'X'], writes=['X'])
            P.op('act', lambda E: E.activation(out=sq[:, :, :], in_=X[:, :, :], func=AF.Square), reads=['X'], writes=['sq'])
            stats(None, list(range(8)), 1.0 / D, None)
            for c in range(8):
                tb = tmp[c % 2]; tk = f"tmp{c % 2}"
                P.op('dve', lambda E, c=c, tb=tb: E.scalar_tensor_tensor(out=tb[:, :], in0=X[:, c, :], scalar=gsc[:, c:c + 1], in1=rstd[:, :],
                                                                         op0=ALU.mult, op1=ALU.mult), reads=['X', 'gsc', 'rstd'], writes=[tk])
                P.op('act', lambda E, c=c, tb=tb: E.activation(out=hT[:, c, :], in_=tb[:, :], func=AF.Identity, bias=vec[:, 4, c:c + 1], scale=1.0),
                     reads=[tk, 'vec'], writes=['hT'])
            for fc in range(32):
                pb, pk = nextps()
                tb = tmp[fc % 2]; tk = f"tmp{fc % 2}"
                for kc in range(8):
                    P.op('pe', lambda E, fc=fc, kc=kc, pb=pb: E.matmul(pb[:, 0:TT], W1[:, kc, fc * 128:(fc + 1) * 128], hT[:, kc, :],
                                                                       start=(kc == 0), stop=(kc == 7)), reads=['W1', 'hT'], writes=[pk])
                P.op('act', lambda E, pb=pb, tb=tb: E.activation(out=tb[:, :], in_=pb[:, 0:TT], func=AF.Relu), reads=[pk], writes=[tk])
                eng = 'dve' if fc % 2 == 0 else 'pool'
                P.op(eng, lambda E, fc=fc, tb=tb: E.tensor_tensor(out=aT[:, fc, :], in0=tb[:, :], in1=tb[:, :], op=ALU.mult),
                     reads=[tk], writes=['aT'])
            for oc in range(8):
                pb, pk = nextps()
                for fc in range(32):
                    P.op('pe', lambda E, oc=oc, fc=fc, pb=pb: E.matmul(pb[:, 0:TT], W2[:, fc, oc * 128:(oc + 1) * 128], aT[:, fc, :],
                                                                       start=(fc == 0), stop=(fc == 31)), reads=['W2', 'aT'], writes=[pk])
                P.op('dve', lambda E, oc=oc, pb=pb: E.scalar_tensor_tensor(out=X[:, oc, :], in0=pb[:, 0:TT], scalar=vec[:, 5, oc:oc + 1],
                                                                           in1=X[:, oc, :], op0=ALU.mult, op1=ALU.add),
                     reads=[pk, 'vec', 'X'], writes=['X'])
            if final:
                P.op('act', lambda E: E.activation(out=sq[:, :, :], in_=X[:, :, :], func=AF.Square), reads=['X'], writes=['sq'])
                stats(None, list(range(8)), 1.0 / D, None)
                for c in range(8):
                    P.op('dve', lambda E, c=c: E.scalar_tensor_tensor(out=X[:, c, :], in0=X[:, c, :], scalar=vec[:, 6, c:c + 1], in1=rstd[:, :],
                                                                      op0=ALU.mult, op1=ALU.mult), reads=['X', 'vec', 'rstd'], writes=['X'])
            P.dma('sp', ov[:, :, t0:t0 + TT], X[:, :, :], reads=['X'])
        P.finish('sp')
        P.emit()
    return nc


_PROGS = {}


def _prog(name, fn, *a):
    key = (name,) + a
    if key not in _PROGS:
        _PROGS[key] = fn(*a)
    return _PROGS[key]


def _run(nc, in_maps):
    res = run_bass_kernel_spmd(nc, in_maps, core_ids=list(range(8)))
    return res.results


def _col(v):
    return np.ascontiguousarray(np.asarray(v, np.float32).reshape(-1, 128).T)


def kernel(x, c, w_ada, b_ada, norm1_gain, norm2_gain, w_in, rel_bias, sinks, forget_bias, lam_re, lam_im,
           log_dt, ssm_b_re, ssm_b_im, ssm_c_re, ssm_c_im, ssm_d, w_glu, b_glu, out_gain, w_out, w_mlp_in,
           w_mlp_out, final_gain):
    import ml_dtypes
    f32 = np.float32
    x = np.asarray(x, f32); c = np.asarray(c, f32)
    args = dict(w_ada=w_ada, b_ada=b_ada, norm1_gain=norm1_gain, norm2_gain=norm2_gain, w_in=w_in, rel_bias=rel_bias,
                sinks=sinks, forget_bias=forget_bias, lam_re=lam_re, lam_im=lam_im, log_dt=log_dt, ssm_b_re=ssm_b_re,
                ssm_b_im=ssm_b_im, ssm_c_re=ssm_c_re, ssm_c_im=ssm_c_im, ssm_d=ssm_d, w_glu=w_glu, b_glu=b_glu,
                out_gain=out_gain, w_out=w_out, w_mlp_in=w_mlp_in, w_mlp_out=w_mlp_out, final_gain=final_gain)
    A = {k: np.asarray(v, f32) for k, v in args.items()}
    B, S, _ = x.shape
    DEPTH = A['w_in'].shape[0]
    NCORE = 8
    CPB = NCORE // B
    T = S // CPB

    ncM = _prog('M', build_stageM)
    ims = []
    for i in range(NCORE):
        l, b = i // B, i % B
        l = min(l, DEPTH - 1)
        ims.append({"cT": _col(c[b]), "wada": A['w_ada'][l], "bada": _col(A['b_ada'][l])})
    rM = _run(ncM, ims)
    mod = {}
    for i in range(NCORE):
        l, b = i // B, i % B
        if l < DEPTH:
            mod[(l, b)] = rM[i]["modv"]

    xT = [np.ascontiguousarray(x[i // CPB, (i % CPB) * T:(i % CPB + 1) * T, :].T) for i in range(NCORE)]
    ncA = _prog('A', build_stageA, T)
    ncSB = _prog('SB', build_sb, S)
    ncSW = _prog('SW', build_swa, S)
    ncFX = _prog('FX', build_fox, S)
    ncSS = _prog('SS', build_ssm, S)
    sbc = sb_consts()
    fxc, fxut = fox_consts()
    bts = [swa_bias_T(A['rel_bias'][:, j]) for j in range(4)]
    for l in range(DEPTH):
        ims = []
        for i in range(NCORE):
            b = i // CPB
            m = mod[(l, b)]
            modv = np.ascontiguousarray(np.stack([_col(A['norm1_gain'][l]), m[:, 8:16], m[:, 0:8]], axis=1))
            ims.append({"xT": xT[i], "win": A['w_in'][l], "modv": modv})
        rA = _run(ncA, ims)
        qk = [np.concatenate([rA[b * CPB + k]["qkT"] for k in range(CPB)], axis=2) for b in range(B)]
        ff = [np.concatenate([rA[b * CPB + k]["fT"] for k in range(CPB)], axis=1) for b in range(B)]
        vt = [np.concatenate([rA[b * CPB + k]["vtok"] for k in range(CPB)], axis=0) for b in range(B)]

        def rows(b, chunk0, j):
            return np.ascontiguousarray(qk[b][chunk0 + j // 2, (j % 2) * 64:(j % 2) * 64 + 64, :])

        im_sb, im_sw, im_fx, im_ss = [], [], [], []
        for i in range(NCORE):
            b, j = i // 4, i % 4
            im_sb.append({"qT": rows(b, 0, j), "kT": rows(b, 2, j), "v": np.ascontiguousarray(vt[b][:, j * 64:(j + 1) * 64]), "cst": sbc})
            kv = j // 2
            im_sw.append({"qT": rows(b, 4, j), "kT": np.ascontiguousarray(qk[b][6, kv * 64:(kv + 1) * 64, :]),
                          "v": np.ascontiguousarray(vt[b][:, 256 + kv * 64:256 + (kv + 1) * 64]), "bt": bts[j],
                          "snk": np.full((128, 1), A['sinks'][l, j], f32)})
            im_fx.append({"qT": rows(b, 7, j), "kT": rows(b, 9, j), "v": np.ascontiguousarray(vt[b][:, 384 + j * 64:384 + (j + 1) * 64]),
                          "ff": np.ascontiguousarray(ff[b][j]), "fb": np.full((128, 1), A['forget_bias'][l, j], f32),
                          "cst": fxc, "ut": fxut})
            d = ssm_inputs(A['lam_re'][l], A['lam_im'][l], A['log_dt'][l], A['ssm_b_re'][l], A['ssm_b_im'][l],
                           A['ssm_c_re'][l], A['ssm_c_im'][l], A['ssm_d'][l], j)
            d["uT"] = rows(b, 11, j)
            im_ss.append(d)
        rSB = _run(ncSB, im_sb)
        rSW = _run(ncSW, im_sw)
        rFX = _run(ncFX, im_fx)
        rSS = _run(ncSS, im_ss)
        mixT = []
        for b in range(B):
            parts = [rSB[b * 4 + j]["o"].T for j in range(4)] + [rSW[b * 4 + j]["o"].T for j in range(4)] + \
                    [rFX[b * 4 + j]["o"].T for j in range(4)] + [rSS[b * 4 + j]["yT"] for j in range(4)]
            mixT.append(np.concatenate(parts, axis=0))
        final = (l == DEPTH - 1)
        ncC = _prog('C', build_stageC, T, final)
        ims = []
        for i in range(NCORE):
            b, k = i // CPB, i % CPB
            m = mod[(l, b)]
            vec = np.ascontiguousarray(np.stack([_col(A['out_gain'][l]), m[:, 16:24], _col(A['norm2_gain'][l]), m[:, 32:40],
                                                 m[:, 24:32], m[:, 40:48], _col(A['final_gain'])], axis=1))
            ims.append({"xT": xT[i], "mixT": np.ascontiguousarray(mixT[b][:, k * T:(k + 1) * T]), "wglu": A['w_glu'][l],
                        "bglu": _col(A['b_glu'][l]), "wout": A['w_out'][l], "w1": A['w_mlp_in'][l], "w2": A['w_mlp_out'][l],
                        "vec": vec})
        rC = _run(ncC, ims)
        xT = [rC[i]["xoT"] for i in range(NCORE)]
    out = np.empty((B, S, D), f32)
    for i in range(NCORE):
        b, k = i // CPB, i % CPB
        out[b, k * T:(k + 1) * T, :] = xT[i].T
    return out
```
